# Optimizing a Trainium2 kernel written in Bass

```python
import jax
import jax.numpy as jnp
from jax import lax
import numpy as np


D_MODEL = 1024
BATCH = 4
SEQ = 4096
DEPTH = 4

CHUNK = 64
HEAD_DIM = 64
D_MIX = D_MODEL
SB_HEADS = 4
SB_WIDTH = SB_HEADS * HEAD_DIM
SB_QBLOCK = 128
CA_HEADS = 4
CA_WIDTH = CA_HEADS * HEAD_DIM
CA_LEFT_CHUNKS = 8
CA_BAND = (CA_LEFT_CHUNKS + 1) * CHUNK
REL_CLIP = 256
REL_TABLE = REL_CLIP + CHUNK
RW_WIDTH = D_MIX - SB_WIDTH - CA_WIDTH
RW_HEADS = RW_WIDTH // HEAD_DIM
D_DECAY_LORA = max(32, int(round(1.8 * D_MODEL ** 0.5 / 32)) * 32)
D_AAA_LORA = max(32, int(round(1.8 * D_MODEL ** 0.5 / 32)) * 32)
D_MV_LORA = max(32, int(round(1.3 * D_MODEL ** 0.5 / 32)) * 32)
D_GATE_LORA = max(32, int(round(0.6 * D_MODEL ** 0.8 / 32)) * 32)
RW_SHIFT_COLS = 3 * RW_WIDTH + D_DECAY_LORA + D_AAA_LORA + D_GATE_LORA
RW_SPLITS = [RW_WIDTH, RW_WIDTH + D_DECAY_LORA, 2 * RW_WIDTH + D_DECAY_LORA,
             3 * RW_WIDTH + D_DECAY_LORA, 3 * RW_WIDTH + D_DECAY_LORA + D_AAA_LORA]
PROJ_SPLITS = [SB_WIDTH, 2 * SB_WIDTH, 3 * SB_WIDTH, 3 * SB_WIDTH + CA_WIDTH,
               3 * SB_WIDTH + 2 * CA_WIDTH, 3 * SB_WIDTH + 3 * CA_WIDTH]
N_IN = PROJ_SPLITS[-1] + RW_SHIFT_COLS
D_FF = 4 * D_MODEL
RMS_EPS = 1e-5
GN_EPS = 64e-5

kernel_name = 'hybrid_stickbreak_chunkrel_rwkv7_encoder'


def rmsnorm(x, g):
    x32 = x.astype(jnp.float32)
    y = x32 * lax.rsqrt(jnp.mean(x32 * x32, axis=-1, keepdims=True) + RMS_EPS)
    return (y * g.astype(jnp.float32)).astype(x.dtype)


def token_shift(x):
    return jnp.pad(x[:, :-1], ((0, 0), (1, 0), (0, 0)))


def stick_breaking_attention(q, k, v):
    B, S, H, d = q.shape
    nq = S // SB_QBLOCK
    scale = d ** -0.5
    k32 = k.astype(jnp.float32)
    v32 = v.astype(jnp.float32)
    key_pos = jnp.arange(S)
    qb = q.astype(jnp.float32).reshape(B, nq, SB_QBLOCK, H, d).transpose(1, 0, 2, 3, 4)

    def block(args):
        qi, i = args
        z = jnp.einsum('bqhd,bshd->bhqs', qi, k32) * scale
        q_pos = i * SB_QBLOCK + jnp.arange(SB_QBLOCK)
        strict = (key_pos[None, :] < q_pos[:, None])[None, None]
        log_1m_beta = jnp.where(strict, jax.nn.log_sigmoid(-z), 0.0)
        after = lax.cumsum(log_1m_beta, axis=3, reverse=True) - log_1m_beta
        weights = jnp.where(strict, jnp.exp(jax.nn.log_sigmoid(z) + after), 0.0)
        return jnp.einsum('bhqs,bshd->bqhd', weights, v32)

    out = lax.map(block, (qb, jnp.arange(nq)))
    return out.transpose(1, 0, 2, 3, 4).reshape(B, S, H * d)


def chunked_relpos_attention(q, k, v, rel_table):
    B, S, H, d = q.shape
    nc = S // CHUNK
    pad = CA_LEFT_CHUNKS * CHUNK
    qc = q.astype(jnp.float32).reshape(B, nc, CHUNK, H, d)
    kp = jnp.pad(k.astype(jnp.float32), ((0, 0), (pad, 0), (0, 0), (0, 0))).reshape(B, nc + CA_LEFT_CHUNKS, CHUNK, H, d)
    vp = jnp.pad(v.astype(jnp.float32), ((0, 0), (pad, 0), (0, 0), (0, 0))).reshape(B, nc + CA_LEFT_CHUNKS, CHUNK, H, d)
    kband = jnp.concatenate([kp[:, o:o + nc] for o in range(CA_LEFT_CHUNKS + 1)], axis=2)
    vband = jnp.concatenate([vp[:, o:o + nc] for o in range(CA_LEFT_CHUNKS + 1)], axis=2)
    s = jnp.einsum('bcqhd,bckhd->bhcqk', qc, kband) * (d ** -0.5)
    qi = np.arange(CHUNK)[:, None]
    kj = np.arange(CA_BAND)[None, :]
    rel = kj - pad - qi
    idx = np.clip(rel, -REL_CLIP, CHUNK - 1) + REL_CLIP
    bias = rel_table.astype(jnp.float32)[:, idx]
    key_abs = np.arange(nc)[:, None] * CHUNK + np.arange(CA_BAND)[None, :] - pad
    valid = (key_abs >= 0)[None, None, :, None, :]
    s = jnp.where(valid, s + bias[None, :, None], -jnp.inf)
    p = jax.nn.softmax(s, axis=-1)
    out = jnp.einsum('bhcqk,bckhd->bcqhd', p, vband)
    return out.reshape(B, S, H * d)


def wkv7_scan(r, w, k, v, a, b):
    B, S, H, N = r.shape

    def step(state, inp):
        r_t, w_t, k_t, v_t, a_t, b_t = inp
        sa = jnp.einsum('bhvk,bhk->bhv', state, a_t)
        state = (state * w_t[:, :, None, :] + sa[..., None] * b_t[:, :, None, :]
                 + v_t[..., None] * k_t[:, :, None, :])
        return state, jnp.einsum('bhvk,bhk->bhv', state, r_t)

    xs = tuple(jnp.moveaxis(t, 1, 0) for t in (r, w, k, v, a, b))
    _, y = lax.scan(step, jnp.zeros((B, H, N, N), jnp.float32), xs)
    return jnp.moveaxis(y, 0, 1)


def rwkv7_time_mix(cols, vd, v_first, mu, w0, w2, a0, a2, v0, v2, g2, k_k, k_a, r_k, ln_w, ln_b):
    B, S, _ = cols.shape
    f32 = jnp.float32
    c = cols.astype(f32)
    c = c + (token_shift(c) - c) * mu.astype(f32)
    r, w_lo, k, v, a_lo, g_lo = jnp.split(c, RW_SPLITS, axis=-1)
    log_w = -jax.nn.softplus(-(w0.astype(f32) + jnp.tanh(w_lo) @ w2.astype(f32))) - 0.5
    decay = jnp.exp(-jnp.exp(log_w))
    if vd is None:
        v_first = v
    else:
        v = v + (v_first - v) * jax.nn.sigmoid(v0.astype(f32) + vd.astype(f32) @ v2.astype(f32))
    a = jax.nn.sigmoid(a0.astype(f32) + a_lo @ a2.astype(f32))
    g = jax.nn.sigmoid(g_lo) @ g2.astype(f32)

    def heads(t):
        return t.reshape(B, S, RW_HEADS, HEAD_DIM)

    kk = heads(k * k_k.astype(f32))
    kk = kk / jnp.maximum(jnp.sqrt(jnp.sum(kk * kk, axis=-1, keepdims=True)), 1e-12)
    k = k * (1.0 + (a - 1.0) * k_a.astype(f32))
    rh, kh, vh, ah = heads(r), heads(k), heads(v), heads(a)
    y = wkv7_scan(rh, heads(decay), kh, vh, -kk, kk * ah)
    mean = jnp.mean(y, axis=-1, keepdims=True)
    var = jnp.mean(jnp.square(y - mean), axis=-1, keepdims=True)
    y = ((y - mean) * lax.rsqrt(var + GN_EPS)).reshape(B, S, RW_WIDTH) * ln_w.astype(f32) + ln_b.astype(f32)
    bonus = jnp.sum(rh * kh * r_k.astype(f32).reshape(RW_HEADS, HEAD_DIM), axis=-1, keepdims=True) * vh
    y = (y + bonus.reshape(B, S, RW_WIDTH)) * g
    return y, v_first


def setup_inputs(seed: int = 0) -> dict:
    key = jax.random.key(seed)
    ks = jax.random.split(key, 26)
    f32 = jnp.float32
    L = DEPTH

    def nrm(k, shape, scale):
        return jax.random.normal(k, shape, f32) * scale

    def gain(k, shape):
        return 1.0 + 0.02 * jax.random.normal(k, shape, f32)

    return {
        'x': nrm(ks[0], (BATCH, SEQ, D_MODEL), 1.0),
        'norm_mix_g': gain(ks[1], (L, D_MODEL)),
        'w_in': nrm(ks[2], (L, D_MODEL, N_IN), D_MODEL ** -0.5),
        'w_vmix_down': nrm(ks[3], (L - 1, D_MODEL, D_MV_LORA), D_MODEL ** -0.5),
        'sb_out_g': gain(ks[4], (L, SB_WIDTH)),
        'ca_rel_bias': nrm(ks[5], (L, CA_HEADS, REL_TABLE), 0.1),
        'ca_out_g': gain(ks[6], (L, CA_WIDTH)),
        'rw_mu': jax.random.uniform(ks[7], (L, RW_SHIFT_COLS), f32),
        'rw_w0': jax.random.uniform(ks[8], (L, RW_WIDTH), f32, -6.5, -1.5),
        'rw_w2': nrm(ks[9], (L, D_DECAY_LORA, RW_WIDTH), 0.1 * D_DECAY_LORA ** -0.5),
        'rw_a0': nrm(ks[10], (L, RW_WIDTH), 0.1),
        'rw_a2': nrm(ks[11], (L, D_AAA_LORA, RW_WIDTH), 0.1 * D_AAA_LORA ** -0.5),
        'rw_v0': 1.0 + nrm(ks[12], (L - 1, RW_WIDTH), 0.1),
        'rw_v2': nrm(ks[13], (L - 1, D_MV_LORA, RW_WIDTH), 0.1 * D_MV_LORA ** -0.5),
        'rw_g2': nrm(ks[14], (L, D_GATE_LORA, RW_WIDTH), D_GATE_LORA ** -0.5),
        'rw_k_k': 0.85 + nrm(ks[15], (L, RW_WIDTH), 0.02),
        'rw_k_a': 1.0 + nrm(ks[16], (L, RW_WIDTH), 0.02),
        'rw_r_k': nrm(ks[17], (L, RW_WIDTH), 0.1),
        'rw_ln_w': gain(ks[18], (L, RW_WIDTH)),
        'rw_ln_b': nrm(ks[19], (L, RW_WIDTH), 0.02),
        'w_out': nrm(ks[20], (L, D_MIX, D_MODEL), D_MIX ** -0.5),
        'norm_ffn_g': gain(ks[21], (L, D_MODEL)),
        'w_ff_in': nrm(ks[22], (L, D_MODEL, D_FF), D_MODEL ** -0.5),
        'w_ff_out': nrm(ks[23], (L, D_FF, D_MODEL), D_FF ** -0.5),
        'norm_final_g': gain(ks[24], (D_MODEL,)),
    }


def reference(x, norm_mix_g, w_in, w_vmix_down, sb_out_g, ca_rel_bias, ca_out_g, rw_mu, rw_w0, rw_w2,
              rw_a0, rw_a2, rw_v0, rw_v2, rw_g2, rw_k_k, rw_k_a, rw_r_k, rw_ln_w, rw_ln_b, w_out,
              norm_ffn_g, w_ff_in, w_ff_out, norm_final_g):
    B, S, _ = x.shape
    v_first = None
    for l in range(DEPTH):
        h = rmsnorm(x, norm_mix_g[l])
        w_proj = w_in[l] if l == 0 else jnp.concatenate([w_in[l], w_vmix_down[l - 1]], axis=1)
        proj = jnp.einsum('bsd,dn->bsn', h, w_proj)
        sb_q, sb_k, sb_v, ca_q, ca_k, ca_v, rest = jnp.split(proj, PROJ_SPLITS, axis=-1)
        rw_cols = rest[..., :RW_SHIFT_COLS]
        vd = None if l == 0 else rest[..., RW_SHIFT_COLS:]

        def sb_heads(t):
            return t.reshape(B, S, SB_HEADS, HEAD_DIM)

        def ca_heads(t):
            return t.reshape(B, S, CA_HEADS, HEAD_DIM)

        sb = rmsnorm(stick_breaking_attention(sb_heads(sb_q), sb_heads(sb_k), sb_heads(sb_v)), sb_out_g[l])
        ca = rmsnorm(chunked_relpos_attention(ca_heads(ca_q), ca_heads(ca_k), ca_heads(ca_v), ca_rel_bias[l]), ca_out_g[l])
        rw, v_first = rwkv7_time_mix(
            rw_cols, vd, v_first, rw_mu[l], rw_w0[l], rw_w2[l], rw_a0[l], rw_a2[l],
            None if l == 0 else rw_v0[l - 1], None if l == 0 else rw_v2[l - 1],
            rw_g2[l], rw_k_k[l], rw_k_a[l], rw_r_k[l], rw_ln_w[l], rw_ln_b[l])
        mix = jnp.concatenate([sb, ca, rw], axis=-1).astype(x.dtype)
        x = x + jnp.einsum('bsm,md->bsd', mix, w_out[l])
        h2 = rmsnorm(x, norm_ffn_g[l])
        f = jnp.square(jax.nn.relu(jnp.einsum('bsd,df->bsf', h2, w_ff_in[l])))
        x = x + jnp.einsum('bsf,fd->bsd', f, w_ff_out[l])
    return rmsnorm(x, norm_final_g)
```

```python
import numpy as np
from contextlib import ExitStack
import concourse.bass as bass
import concourse.mybir as mybir

F32 = mybir.dt.float32
BF16 = mybir.dt.bfloat16
AF = mybir.ActivationFunctionType
ALU = mybir.AluOpType
AX = mybir.AxisListType

NDSEM = 8


class Tok:
    __slots__ = ("name", "w", "rs", "rd")

    def __init__(self, name=""):
        self.name = name
        self.w = None
        self.rs = {}
        self.rd = []


class XTok:
    __slots__ = ("name", "last")

    def __init__(self, name=""):
        self.name = name
        self.last = {}


class Op:
    __slots__ = ("st", "idx", "fn", "deps", "sig", "dma", "val", "sem")

    def __init__(self, st, idx, fn, dma):
        self.st = st
        self.idx = idx
        self.fn = fn
        self.deps = []
        self.sig = False
        self.dma = dma
        self.val = None
        self.sem = None


class Sched:
    STREAMS = ["pe", "act", "dve", "pool", "sp"]

    def __init__(self, nc, es):
        self.nc = nc
        self.es = es
        self.ops = {s: [] for s in self.STREAMS}
        self.csem = {s: es.enter_context(nc.semaphore("c_" + s)) for s in self.STREAMS}
        self.dsem = {s: [es.enter_context(nc.semaphore("d_%s%d" % (s, k))) for k in range(NDSEM)]
                     for s in ("sp", "pool", "act")}
        self.dmas = {s: [] for s in ("sp", "pool", "act")}
        self.n_ops = 0

    def sb(self, name, shape, dt):
        return self.es.enter_context(self.nc.sbuf_tensor("sb_" + name, list(shape), dt))

    def ps(self, name, shape, dt=F32):
        return self.es.enter_context(self.nc.psum_tensor("ps_" + name, list(shape), dt))

    def _add(self, st, fn, r, w, dma, x=()):
        op = Op(st, len(self.ops[st]), fn, dma)
        deps = []
        for t in x:
            for st2, o2 in t.last.items():
                if st2 != st:
                    deps.append(o2)
            t.last[st] = op
        for t in r:
            if t.w is not None:
                deps.append(t.w)
        for t in w:
            if t.w is not None:
                deps.append(t.w)
            deps.extend(t.rs.values())
            deps.extend(t.rd)
        if dma:
            lst = self.dmas[st]
            if len(lst) >= NDSEM:
                deps.append(lst[len(lst) - NDSEM])
            op.sig = True
            k = len(lst)
            op.sem = self.dsem[st][k % NDSEM]
            op.val = 16 * (k // NDSEM + 1)
            lst.append(op)
        seen = set()
        for d in deps:
            if d is op or id(d) in seen:
                continue
            seen.add(id(d))
            if (not d.dma) and d.st == st and st == "pe":
                continue
            op.deps.append(d)
            d.sig = True
        for t in r:
            if dma:
                t.rd.append(op)
            else:
                t.rs[st] = op
        for t in w:
            t.w = op
            t.rs = {}
            t.rd = []
        self.ops[st].append(op)
        self.n_ops += 1
        return op

    def op(self, st, meth, *args, r=(), w=(), x=(), **kw):
        return self._add(st, lambda e: getattr(e, meth)(*args, **kw), r, w, False, x)

    def dma(self, st, out, in_, r=(), w=()):
        return self._add(st, lambda e: e.dma_start(out=out, in_=in_), r, w, True)

    def mm(self, out, lhsT, rhs, start, stop, r=(), w=(), x=(), **kw):
        return self._add("pe", lambda e: e.matmul(out, lhsT, rhs, start=start, stop=stop, **kw), r, w, False, x)

    def finalize(self, final_ops=()):
        nc = self.nc
        for st in self.STREAMS:
            c = 0
            for op in self.ops[st]:
                if op.dma:
                    continue
                if op.sig:
                    c += 1
                    op.val = c
                    op.sem = self.csem[st]
        eng_of = {"pe": "tensor", "act": "scalar", "dve": "vector", "pool": "gpsimd", "sp": "sync"}

        def emit(st, e):
            known = {}
            for op in self.ops[st]:
                need = {}
                for d in op.deps:
                    key = id(d.sem)
                    if key not in need or need[key][1] < d.val:
                        need[key] = (d.sem, d.val)
                for key, (sem, val) in need.items():
                    if known.get(key, 0) >= val:
                        continue
                    e.wait_ge(sem, val)
                    known[key] = val
                ins = op.fn(e)
                if op.sig:
                    ins.then_inc(op.sem, 16 if op.dma else 1)
            for d in final_ops.get(st, []) if isinstance(final_ops, dict) else []:
                e.wait_ge(d.sem, d.val)

        with nc.Block() as block:
            for st in self.STREAMS:
                getattr(block, eng_of[st])(lambda e, st=st: emit(st, e))


TT = 512
HT = 256
D = 1024
RMS_EPS = 1e-5
GN_EPS = 64e-5
NCOLS = 12 * 128 + 64 + 256
CDEC = float(np.exp(-0.5))
NCST = 25
PV_G, PV_MU, PV_OMU, PV_W0, PV_A0, PV_V0, PV_KK, PV_KA, PV_OKA, PV_RK, PV_LNW, PV_LNB = 0, 8, 17, 26, 28, 30, 32, 34, 36, 38, 40, 42
NPV = 64


def build_phase_a(S, first, parts=("sb", "ca", "rw")):
    nc = bass.Bass("TRN2", target_bir_lowering=False)
    a = PA(nc, S, first, parts)
    return nc


class PA:
    def __init__(self, nc, S, first, parts):
        self.nc, self.S, self.first, self.parts = nc, S, first, parts
        self.ntile = S // TT
        dt = nc.dram_tensor
        self.xT_d = dt("xT", [D, S], F32, kind="ExternalInput").ap()
        self.w_d = dt("w_core", [D, NCOLS], F32, kind="ExternalInput").ap()
        self.pv_d = dt("pvec", [128, NPV], F32, kind="ExternalInput").ap()
        self.lw_d = dt("lora_w", [128, 3, 256], F32, kind="ExternalInput").ap()
        self.cab_d = dt("ca_bias", [128, 2, 5, 128], F32, kind="ExternalInput").ap()
        self.cst_d = dt("consts", [128, NCST * 128], F32, kind="ExternalInput").ap()
        self.vf_in = None if first else dt("vfirst_in", [128, 2, S], F32, kind="ExternalInput").ap()
        self.out_d = dt("mixhT", [512, S], F32, kind="ExternalOutput").ap()
        self.vf_out = dt("vfirst_out", [128, 2, S], F32, kind="ExternalOutput").ap() if first else None
        with ExitStack() as es:
            self.s = Sched(nc, es)
            self.fin = []
            self.setup()
            for j in range(self.ntile):
                self.tile(j)
            self.s.finalize({"sp": self.fin})

    def setup(self):
        s, S = self.s, self.S
        T = Tok
        self.pv = s.sb("pv", [128, NPV], F32); self.pv_t = T()
        s.dma("sp", self.pv[:], self.pv_d, w=[self.pv_t])
        self.cst = s.sb("cst", [128, NCST * 128], BF16); self.cst_t = T()
        s.dma("pool", self.cst[:], self.cst_d, w=[self.cst_t])
        self.cst32 = self.cst; self.cst32_t = self.cst_t
        self.eps = s.sb("eps", [128, 2], F32); self.eps_t = T()
        s.op("pool", "memset", self.eps[:, 0:1], RMS_EPS, w=[self.eps_t])
        s.op("pool", "memset", self.eps[:, 1:2], GN_EPS, w=[self.eps_t])
        self.W = s.sb("W", [128, 8, NCOLS], BF16); self.W_t = [T() for kc in range(8)]
        wv = self.w_d.rearrange("(kc p) n -> p kc n", p=128)
        for kc in range(8):
            s.dma("pool", self.W[:, kc, :], wv[:, kc, :], w=[self.W_t[kc]])
        self.LW = s.sb("LW", [128, 3, 256], BF16); self.LW_t = T()
        s.dma("pool", self.LW[:], self.lw_d, w=[self.LW_t])
        self.cab = s.sb("cab", [128, 2, 5, 128], BF16); self.cab_t = T()
        s.dma("pool", self.cab[:], self.cab_d, w=[self.cab_t])
        self.xt = s.sb("xt", [128, 8, TT], F32); self.xt_t = [T() for c in range(8)]
        self.hT = s.sb("hT", [128, 8, TT], BF16); self.hT_t = [T() for c in range(8)]
        self.sq = [s.sb("sq%d" % i, [128, TT], BF16) for i in range(2)]; self.sq_t = [T(), T()]
        self.rstd = s.sb("rstd", [128, TT], F32); self.rstd_t = T()
        nt = self.ntile
        self.qsb = s.sb("qsb", [128, TT], BF16); self.qsb_t = T()
        self.ksb = s.sb("ksb", [128, S], BF16); self.ksb_t = [T() for j in range(nt)]
        self.vsb = s.sb("vsb", [128, S // 128, 128], BF16); self.vsb_t = [T() for j in range(nt)]
        self.qca = s.sb("qca", [128, TT], BF16); self.qca_t = T()
        self.kca = s.sb("kca", [128, S], BF16); self.kca_t = [T() for j in range(nt)]
        self.vca = s.sb("vca", [128, S // 128, 128], BF16); self.vca_t = [T() for j in range(nt)]
        self.outst = s.sb("outst", [128, 4, TT], F32); self.outst_t = [T() for c in range(4)]
        self.fb = [s.ps("fb%d" % i, [128, TT]) for i in range(6)]
        self.fx = [XTok("fb%d" % i) for i in range(6)]
        self.bb = [s.ps("bb%d" % i, [128, 2 * TT], BF16) for i in range(2)]
        self.bx = [XTok("bb%d" % i) for i in range(2)]
        for i in range(6):
            s.op("dve", "memset", self.fb[i][:], 0.0, x=[self.fx[i]])
        self.gp = 0
        s.op("dve", "tensor_scalar", self.pv[:, PV_OMU:PV_OMU + 9], self.pv[:, PV_MU:PV_MU + 9], -1.0, 1.0, ALU.mult,
             ALU.add, r=[self.pv_t], w=[self.pv_t])
        s.op("dve", "tensor_scalar", self.pv[:, PV_OKA:PV_OKA + 2], self.pv[:, PV_KA:PV_KA + 2], -1.0, 1.0, ALU.mult,
             ALU.add, r=[self.pv_t], w=[self.pv_t])
        if "sb" in self.parts:
            self.sb_setup()
        if "ca" in self.parts:
            self.ca_setup()
        if "rw" in self.parts:
            self.rw_setup()

    def c32(self, i, n=128):
        return self.cst32[:, i * 128:i * 128 + n]

    def cb(self, i, n=128):
        return self.cst[:, i * 128:i * 128 + n]

    def pvc(self, i):
        return self.pv[:, i:i + 1]

    def nextbank(self):
        i = self.gp
        self.gp = (i + 1) % 2
        return self.fb[i], self.fx[i]

    def tile(self, j):
        s, parts = self.s, self.parts
        sl = slice(j * TT, (j + 1) * TT)
        xv = self.xT_d.rearrange("(c p) n -> p c n", p=128)
        for c in range(8):
            s.dma("sp", self.xt[:, c, :], xv[:, c, sl], w=[self.xt_t[c]])
        pn, pnx = self.nextbank()
        ones = self.cb(0)
        for c in range(8):
            s.op("act", "activation", self.sq[c % 2][:], self.xt[:, c, :], AF.Square, r=[self.xt_t[c]],
                 w=[self.sq_t[c % 2]])
            s.mm(pn[:], ones, self.sq[c % 2][:], c == 0, c == 7, r=[self.cst_t, self.sq_t[c % 2]], x=[pnx])
        s.op("act", "activation", self.rstd[:], pn[:], AF.Sqrt, bias=self.eps[:, 0:1], scale=1.0 / D,
             r=[self.eps_t], w=[self.rstd_t], x=[pnx])
        s.op("dve", "reciprocal", self.rstd[:], self.rstd[:], r=[self.rstd_t], w=[self.rstd_t])
        for c in range(8):
            s.op("dve", "scalar_tensor_tensor", self.hT[:, c, :], self.xt[:, c, :], self.pvc(PV_G + c), self.rstd[:],
                 ALU.mult, ALU.mult, r=[self.xt_t[c], self.rstd_t, self.pv_t], w=[self.hT_t[c]])
        if {"sb", "ca"} & set(parts):
            for ch in range(4):
                pb, px = self.nextbank()
                for kc in range(8):
                    s.mm(pb[:], self.W[:, kc, ch * 128:(ch + 1) * 128], self.hT[:, kc, :], kc == 0, kc == 7,
                         r=[self.W_t[kc], self.hT_t[kc]], x=[px])
                if ch == 0:
                    s.op("act", "mul", self.qsb[:], pb[:], 0.125, w=[self.qsb_t], x=[px])
                elif ch == 1:
                    s.op("act", "copy", self.ksb[:, sl], pb[:], w=[self.ksb_t[j]], x=[px])
                elif ch == 2:
                    s.op("act", "mul", self.qca[:], pb[:], 0.125, w=[self.qca_t], x=[px])
                else:
                    s.op("act", "copy", self.kca[:, sl], pb[:], w=[self.kca_t[j]], x=[px])
            for t4 in range(4):
                pb, px = self.nextbank()
                for kc in range(8):
                    s.mm(pb[:, 0:256], self.hT[:, kc, t4 * 128:(t4 + 1) * 128], self.W[:, kc, 1600:1856], kc == 0,
                         kc == 7, r=[self.W_t[kc], self.hT_t[kc]], x=[px])
                kt = j * 4 + t4
                s.op("act", "copy", self.vsb[:, kt, :], pb[:, 0:128], w=[self.vsb_t[j]], x=[px])
                s.op("act", "copy", self.vca[:, kt, :], pb[:, 128:256], w=[self.vca_t[j]], x=[px])
        if "sb" in parts:
            self.sb_attention(j)
        if "ca" in parts:
            self.ca_attention(j)
        if "rw" in parts:
            for half in range(2):
                self.rw_half(j, half)
        ov = self.out_d.rearrange("(c p) n -> p c n", p=128)
        for c in range(4):
            if self.outst_t[c].w is not None:
                self.fin.append(s.dma("sp", ov[:, c, sl], self.outst[:, c, :], r=[self.outst_t[c]]))

    def sb_setup(self):
        s = self.s
        T = Tok
        self.sbe = [s.sb("sb_e%d" % i, [128, TT], F32) for i in range(2)]; self.sbe_t = [T(), T()]
        self.sbsp = [s.sb("sb_sp%d" % i, [128, TT], BF16) for i in range(2)]; self.sbsp_t = [T(), T()]
        self.sbsum = s.sb("sb_ssum", [128, 2, TT], F32); self.sbsum_t = [T(), T()]
        self.sbssb = [s.sb("sb_ssb%d" % i, [128, TT], BF16) for i in range(2)]; self.sbssb_t = [T(), T()]
        self.sbw = [s.sb("sb_w%d" % i, [128, TT], BF16) for i in range(2)]; self.sbw_t = [T(), T()]
        self.negones = s.sb("sb_negones", [128, 128], BF16); self.negones_t = T()
        s.op("pool", "memset", self.negones[:], -1.0, w=[self.negones_t])

    def sb_attention(self, j):
        s = self.s
        negU = self.cb(4)
        lmask = self.c32(5)
        OT, OTx = self.fb[4], self.fx[4]
        nk = 4 * (j + 1)
        it = 0
        for h in range(2):
            hp = slice(64 * h, 64 * h + 64)
            s.op("pool", "memset", self.sbsum[:, h, :], 0.0, w=[self.sbsum_t[h]])
            for kt in range(nk - 1, -1, -1):
                o = kt * 128 - j * TT
                c0 = max(o, 0)
                cs = slice(c0, TT)
                diag = o >= 0
                jt = kt // 4
                i2 = it % 2
                it += 1
                Z, Zx = self.fb[2 + i2], self.fx[2 + i2]
                C, Cx = self.fb[i2], self.fx[i2]
                eb, eb_t = self.sbe[i2], self.sbe_t[i2]
                spb, spb_t = self.sbsp[i2], self.sbsp_t[i2]
                ssb, ssb_t = self.sbssb[i2], self.sbssb_t[i2]
                wb, wb_t = self.sbw[i2], self.sbw_t[i2]
                kT = self.ksb[hp, kt * 128:(kt + 1) * 128]
                q = self.qsb[hp, cs]
                s.mm(Z[:, cs], kT, q, True, True, r=[self.ksb_t[jt], self.qsb_t], x=[Zx])
                s.op("act", "activation", eb[:, cs], Z[:, cs], AF.Exp, w=[eb_t], x=[Zx])
                s.op("act", "activation", spb[:, cs], eb[:, cs], AF.Ln, bias=1.0, r=[eb_t], w=[spb_t])
                if diag:
                    dsl = slice(c0, c0 + 128)
                    s.op("dve", "tensor_tensor", spb[:, dsl], spb[:, dsl], lmask, ALU.mult,
                         r=[spb_t, self.cst32_t], w=[spb_t])
                s.op("pool", "tensor_copy", ssb[:, cs], self.sbsum[:, h, cs], r=[self.sbsum_t[h]], w=[ssb_t])
                s.mm(C[:, cs], negU, spb[:, cs], True, False, r=[self.cst_t, spb_t], x=[Cx])
                s.mm(C[:, cs], self.negones[:], ssb[:, cs], False, False, r=[self.negones_t, ssb_t], x=[Cx])
                s.mm(C[:, cs], kT, q, False, True, r=[self.ksb_t[jt], self.qsb_t], x=[Cx])
                s.op("act", "activation", wb[:, cs], C[:, cs], AF.Exp, w=[wb_t], x=[Cx])
                if diag:
                    s.op("dve", "tensor_tensor", wb[:, dsl], wb[:, dsl], lmask, ALU.mult,
                         r=[wb_t, self.cst32_t], w=[wb_t])
                s.op("pool", "tensor_tensor", self.sbsum[:, h, cs], self.sbsum[:, h, cs], spb[:, cs], ALU.add,
                     r=[spb_t, self.sbsum_t[h]], w=[self.sbsum_t[h]])
                s.mm(OT[hp, cs], self.vsb[:, kt, hp], wb[:, cs], kt == nk - 1, kt == 0,
                     r=[self.vsb_t[jt], wb_t], x=[OTx], skip_group_check=True)
        s.op("act", "copy", self.outst[:, 0, :], OT[:], w=[self.outst_t[0]], x=[OTx])

    def ca_setup(self):
        s = self.s
        self.cap = [s.sb("ca_p%d" % i, [128, 128], BF16) for i in range(4)]
        self.cap_t = [Tok() for i in range(4)]
        self.carec = s.sb("ca_rec", [128, TT], F32); self.carec_t = Tok()

    def ca_attention(self, j):
        s = self.s
        ident = self.cb(1)
        ones64 = self.cst[:, 0:64]
        OT, OTx = self.fb[4], self.fx[4]
        DN, DNx = self.fb[5], self.fx[5]
        it = 0
        for gq in range(4):
            G = 4 * j + gq
            qs = slice(gq * 128, (gq + 1) * 128)
            kts = [kt for kt in range(G - 4, G + 1) if kt >= 0]
            for h in range(2):
                hp = slice(64 * h, 64 * h + 64)
                for kt in kts:
                    bl = kt - (G - 4)
                    jt = kt // 4
                    i4 = it % 4
                    it += 1
                    Sb, Sx = self.fb[i4], self.fx[i4]
                    pT, pT_t = self.cap[i4], self.cap_t[i4]
                    s.mm(Sb[:, 0:128], self.kca[hp, kt * 128:(kt + 1) * 128], self.qca[hp, qs], True, False,
                         r=[self.kca_t[jt], self.qca_t], x=[Sx])
                    s.mm(Sb[:, 0:128], ident, self.cab[:, h, bl, :], False, True, r=[self.cst_t, self.cab_t], x=[Sx])
                    s.op("act", "activation", pT[:], Sb[:, 0:128], AF.Exp, w=[pT_t], x=[Sx])
                    s.mm(OT[hp, qs], self.vca[:, kt, hp], pT[:], kt == kts[0], kt == kts[-1],
                         r=[self.vca_t[jt], pT_t], x=[OTx], skip_group_check=True)
                    s.mm(DN[hp, qs], ones64, pT[:], kt == kts[0], kt == kts[-1], r=[self.cst_t, pT_t], x=[DNx],
                         skip_group_check=True)
        s.op("dve", "reciprocal", self.carec[:], DN[:], w=[self.carec_t], x=[DNx])
        s.op("dve", "tensor_tensor", self.outst[:, 1, :], OT[:], self.carec[:], ALU.mult, r=[self.carec_t],
             w=[self.outst_t[1]], x=[OTx])

    def rw_setup(self):
        s = self.s
        T = Tok
        f = lambda n: (s.sb(n, [128, HT], F32), T())
        b = lambda n: (s.sb(n, [128, HT], BF16), T())
        self.rawrot = [s.sb("rawrot%d" % i, [128, HT + 1], F32) for i in range(2)]; self.rawrot_t = [T(), T()]
        self.carry = s.sb("carry", [128, 9], F32); self.carry_t = [T() for c in range(9)]
        s.op("pool", "memset", self.carry[:], 0.0, w=self.carry_t)
        self.tmpm = [s.sb("tmpm%d" % i, [128, HT], F32) for i in range(2)]; self.tmpm_t = [T(), T()]
        self.MX = s.sb("MX", [128, 9, HT], F32); self.MX_t = [T() for c in range(9)]
        self.VFt = s.sb("VFt", [128, 2, HT], F32); self.VFt_t = [T(), T()]
        self.rmask = s.sb("rmask", [128, HT], F32); self.rmask_t = T()
        s.op("pool", "memset", self.rmask[:], 1.0, w=[self.rmask_t])
        for cc in range(HT // 64):
            s.op("pool", "memset", self.rmask[:, cc * 64:cc * 64 + 1], 0.0, w=[self.rmask_t])
        self.TWA, self.TWA_t = b("TWA")
        self.SGL, self.SGL_t = b("SGL")
        self.G1b, self.G1b_t = b("G1b")
        names = ["SG", "A", "NU", "dv", "CS", "ginv", "CSe", "ge", "KK", "nrm", "KKn", "T1", "Km", "Bs"]
        self.t32 = {}
        for n in names:
            self.t32[n] = f("rw_" + n)
        self.KK2, self.KK2_t = b("KK2")
        self.rk, self.rk_t = b("rk")
        self.GATE = s.sb("GATE", [128, 2, HT], F32); self.GATE_t = [T(), T()]
        self.GI = s.sb("GI", [128, 2, HT], F32); self.GI_t = [T(), T()]
        self.BON = s.sb("BON", [128, 2, HT], F32); self.BON_t = [T(), T()]
        self.AT = s.sb("AT", [128, 2, HT], BF16); self.AT_t = [T(), T()]
        self.BT = s.sb("BT", [128, 2, HT], BF16); self.BT_t = [T(), T()]
        self.KT = s.sb("KT", [128, 2, HT], BF16); self.KT_t = [T(), T()]
        self.RT = s.sb("RT", [128, 2, HT], BF16); self.RT_t = [T(), T()]
        self.Vbf = s.sb("Vbf", [128, 2, HT], BF16); self.Vbf_t = [T(), T()]
        NCK = HT // 64
        self.NCK = NCK
        self.TOKS = s.sb("TOKS", [128, NCK, 2, 3, 64], BF16)
        self.TOKS_t = [[T() for p in range(2)] for c in range(NCK)]
        self.MA = s.sb("MA", [128, NCK, 2, 256], BF16); self.MA_t = [[T() for p in range(2)] for c in range(NCK)]
        self.MB = s.sb("MB", [128, NCK, 2, 192], BF16); self.MB_t = [[T() for p in range(2)] for c in range(NCK)]
        self.Pm = s.sb("Pm", [128, NCK, 2, 256], BF16); self.Pm_t = [[T() for p in range(2)] for c in range(NCK)]
        self.OG = [s.sb("OG%d" % i, [128, 256], BF16) for i in range(2)]; self.OG_t = [T(), T()]
        self.WW = [s.sb("WW%d" % i, [128, 256], BF16) for i in range(2)]; self.WW_t = [T(), T()]
        self.H = s.sb("H", [128, 2, 64], F32); self.H_t = [T(), T()]
        self.Hbf = s.sb("Hbf", [128, 2, 2, 64], BF16); self.Hbf_t = [[T(), T()], [T(), T()]]
        self.HG = s.sb("HG", [128, 2, 64], F32); self.HG_t = [T(), T()]
        self.Zs = s.sb("Zs", [128, 2, 64], BF16); self.Zs_t = [T(), T()]
        self.Us = s.sb("Us", [128, 2, 64], BF16); self.Us_t = [T(), T()]
        for p in range(2):
            s.op("pool", "memset", self.H[:, p, :], 0.0, w=[self.H_t[p]])
            s.op("pool", "memset", self.Hbf[:, p, 0, :], 0.0, w=[self.Hbf_t[p][0]])
        self.hpar = 0

    def rw_half(self, j, half):
        s = self.s
        first = self.first
        t32 = self.t32
        c0 = half * HT
        gsl = slice(j * TT + c0, j * TT + c0 + HT)
        hsl = slice(c0, c0 + HT)
        for c in range(9):
            ch = 4 + c
            m = 128 if c < 8 else 64
            pb, px = self.nextbank()
            for kc in range(8):
                s.mm(pb[0:m, 0:HT], self.W[:, kc, ch * 128:ch * 128 + m], self.hT[:, kc, hsl], kc == 0, kc == 7,
                     r=[self.W_t[kc], self.hT_t[kc]], x=[px])
            rr, rr_t = self.rawrot[c % 2], self.rawrot_t[c % 2]
            tm, tm_t = self.tmpm[c % 2], self.tmpm_t[c % 2]
            s.op("pool", "tensor_copy", rr[0:m, 0:1], self.carry[0:m, c:c + 1], r=[self.carry_t[c]], w=[rr_t])
            s.op("act", "copy", rr[0:m, 1:HT + 1], pb[0:m, 0:HT], w=[rr_t], x=[px])
            s.op("pool", "tensor_scalar", tm[0:m, :], rr[0:m, 0:HT], self.pv[0:m, PV_MU + c:PV_MU + c + 1], 0.0,
                 ALU.mult, ALU.add, r=[rr_t, self.pv_t], w=[tm_t])
            s.op("dve", "scalar_tensor_tensor", self.MX[0:m, c, :], rr[0:m, 1:HT + 1],
                 self.pv[0:m, PV_OMU + c:PV_OMU + c + 1], tm[0:m, :], ALU.mult, ALU.add,
                 r=[rr_t, tm_t, self.pv_t], w=[self.MX_t[c]])
            s.op("pool", "tensor_copy", self.carry[0:m, c:c + 1], rr[0:m, HT:HT + 1], r=[rr_t], w=[self.carry_t[c]])
        MX, MX_t = self.MX, self.MX_t
        s.op("act", "activation", self.TWA[0:64, :], MX[0:64, 6, :], AF.Tanh, r=[MX_t[6]], w=[self.TWA_t])
        s.op("act", "copy", self.TWA[64:128, :], MX[64:128, 6, :], r=[MX_t[6]], w=[self.TWA_t])
        s.op("act", "activation", self.SGL[:], MX[:, 7, :], AF.Sigmoid, r=[MX_t[7]], w=[self.SGL_t])
        s.op("act", "activation", self.G1b[0:32, :], MX[0:32, 8, :], AF.Sigmoid, r=[MX_t[8]], w=[self.G1b_t])
        s.op("act", "copy", self.G1b[32:64, :], MX[32:64, 8, :], r=[MX_t[8]], w=[self.G1b_t])
        if not first:
            for p in range(2):
                s.dma("sp", self.VFt[:, p, :], self.vf_in[:, p, gsl], w=[self.VFt_t[p]])
        blk1 = self.cb(3)
        for p in range(2):
            ps = slice(p * 128, (p + 1) * 128)
            R, R_t = MX[:, p, :], MX_t[p]
            Kx, Kx_t = MX[:, 2 + p, :], MX_t[2 + p]
            Vx, Vx_t = MX[:, 4 + p, :], MX_t[4 + p]
            SG, SG_t = t32["SG"]; A, A_t = t32["A"]; NU, NU_t = t32["NU"]; dv, dv_t = t32["dv"]
            CS, CS_t = t32["CS"]; ginv, ginv_t = t32["ginv"]; CSe, CSe_t = t32["CSe"]; ge, ge_t = t32["ge"]
            KK, KK_t = t32["KK"]; nrm, nrm_t = t32["nrm"]; KKn, KKn_t = t32["KKn"]; T1, T1_t = t32["T1"]
            Km, Km_t = t32["Km"]; Bs, Bs_t = t32["Bs"]
            pb, px = self.nextbank()
            s.mm(pb[:, 0:HT], self.LW[0:64, 0, ps], self.TWA[0:64, :], True, True, r=[self.LW_t, self.TWA_t], x=[px])
            s.op("act", "activation", SG[:], pb[:, 0:HT], AF.Sigmoid, bias=self.pvc(PV_W0 + p), r=[self.pv_t],
                 w=[SG_t], x=[px])
            pb, px = self.nextbank()
            s.mm(pb[:, 0:HT], self.LW[64:128, 0, ps], self.TWA[64:128, :], True, True, r=[self.LW_t, self.TWA_t],
                 x=[px])
            s.op("act", "activation", A[:], pb[:, 0:HT], AF.Sigmoid, bias=self.pvc(PV_A0 + p), r=[self.pv_t],
                 w=[A_t], x=[px])
            pb, px = self.nextbank()
            s.mm(pb[:, 0:HT], self.LW[:, 1, ps], self.SGL[:], True, False, r=[self.LW_t, self.SGL_t], x=[px])
            s.mm(pb[:, 0:HT], self.LW[0:32, 2, ps], self.G1b[0:32, :], False, True, r=[self.LW_t, self.G1b_t], x=[px])
            s.op("act", "copy", self.GATE[:, p, :], pb[:, 0:HT], w=[self.GATE_t[p]], x=[px])
            if not first:
                pb, px = self.nextbank()
                s.mm(pb[:, 0:HT], self.LW[32:64, 2, ps], self.G1b[32:64, :], True, True,
                     r=[self.LW_t, self.G1b_t], x=[px])
                s.op("act", "activation", NU[:], pb[:, 0:HT], AF.Sigmoid, bias=self.pvc(PV_V0 + p), r=[self.pv_t],
                     w=[NU_t], x=[px])
                s.op("pool", "tensor_tensor", dv[:], self.VFt[:, p, :], Vx, ALU.subtract, r=[self.VFt_t[p], Vx_t],
                     w=[dv_t])
                s.op("pool", "tensor_tensor", dv[:], dv[:], NU[:], ALU.mult, r=[dv_t, NU_t], w=[dv_t])
                s.op("dve", "tensor_tensor", Vx, Vx, dv[:], ALU.add, r=[Vx_t, dv_t], w=[Vx_t])
            else:
                s.op("pool", "tensor_copy", self.VFt[:, p, :], Vx, r=[Vx_t], w=[self.VFt_t[p]])
                self.fin.append(s.dma("sp", self.vf_out[:, p, gsl], self.VFt[:, p, :], r=[self.VFt_t[p]]))
            s.op("dve", "tensor_tensor_scan", CS[:], self.rmask[:], SG[:], 0.0, ALU.mult, ALU.add,
                 r=[self.rmask_t, SG_t], w=[CS_t])
            s.op("act", "activation", self.GI[:, p, :], CS[:], AF.Exp, scale=-CDEC, r=[CS_t], w=[self.GI_t[p]])
            s.op("act", "activation", ginv[:], CS[:], AF.Exp, scale=CDEC, r=[CS_t], w=[ginv_t])
            s.op("pool", "tensor_tensor", CSe[:], CS[:], SG[:], ALU.subtract, r=[CS_t, SG_t], w=[CSe_t])
            s.op("act", "activation", ge[:], CSe[:], AF.Exp, scale=-CDEC, r=[CSe_t], w=[ge_t])
            s.op("pool", "tensor_scalar", KK[:], Kx, self.pvc(PV_KK + p), 0.0, ALU.mult, ALU.add,
                 r=[Kx_t, self.pv_t], w=[KK_t])
            s.op("act", "activation", self.KK2[:], KK[:], AF.Square, r=[KK_t], w=[self.KK2_t])
            pb, px = self.nextbank()
            s.mm(pb[:, 0:HT], blk1, self.KK2[:], True, True, r=[self.cst_t, self.KK2_t], x=[px])
            s.op("act", "activation", nrm[:], pb[:, 0:HT], AF.Sqrt, w=[nrm_t], x=[px])
            s.op("dve", "tensor_scalar_max", nrm[:], nrm[:], 1e-12, r=[nrm_t], w=[nrm_t])
            s.op("dve", "reciprocal", nrm[:], nrm[:], r=[nrm_t], w=[nrm_t])
            s.op("pool", "tensor_tensor", KKn[:], KK[:], nrm[:], ALU.mult, r=[KK_t, nrm_t], w=[KKn_t])
            s.op("dve", "tensor_scalar", T1[:], A[:], self.pvc(PV_KA + p), self.pvc(PV_OKA + p), ALU.mult, ALU.add,
                 r=[A_t, self.pv_t], w=[T1_t])
            s.op("pool", "tensor_tensor", Km[:], Kx, T1[:], ALU.mult, r=[Kx_t, T1_t], w=[Km_t])
            s.op("dve", "scalar_tensor_tensor", self.AT[:, p, :], KKn[:], -1.0, ge[:], ALU.mult, ALU.mult,
                 r=[KKn_t, ge_t], w=[self.AT_t[p]])
            s.op("pool", "tensor_tensor", Bs[:], KKn[:], A[:], ALU.mult, r=[KKn_t, A_t], w=[Bs_t])
            s.op("dve", "tensor_tensor", self.BT[:, p, :], Bs[:], ginv[:], ALU.mult, r=[Bs_t, ginv_t],
                 w=[self.BT_t[p]])
            s.op("dve", "tensor_tensor", self.KT[:, p, :], Km[:], ginv[:], ALU.mult, r=[Km_t, ginv_t],
                 w=[self.KT_t[p]])
            s.op("dve", "tensor_tensor", self.RT[:, p, :], R, self.GI[:, p, :], ALU.mult, r=[R_t, self.GI_t[p]],
                 w=[self.RT_t[p]])
            s.op("act", "copy", self.Vbf[:, p, :], Vx, r=[Vx_t], w=[self.Vbf_t[p]])
            s.op("dve", "scalar_tensor_tensor", self.rk[:], R, self.pvc(PV_RK + p), Km[:], ALU.mult, ALU.mult,
                 r=[R_t, Km_t, self.pv_t], w=[self.rk_t])
            pb, px = self.nextbank()
            s.mm(pb[:, 0:HT], blk1, self.rk[:], True, True, r=[self.cst_t, self.rk_t], x=[px])
            s.op("dve", "tensor_tensor", self.BON[:, p, :], pb[:, 0:HT], Vx, ALU.mult, r=[Vx_t], w=[self.BON_t[p]],
                 x=[px])
        NCK = self.NCK
        ident = self.cb(1)
        maskA = self.cst32[:, 6 * 128:8 * 128]
        maskB = self.cst32[:, 9 * 128:9 * 128 + 192]
        for cc in range(NCK):
            csl = slice(cc * 64, cc * 64 + 64)
            bbk, bbx = self.bb[cc % 2], self.bx[cc % 2]
            for p in range(2):
                for ti, (X, X_t) in enumerate(((self.BT, self.BT_t), (self.KT, self.KT_t), (self.Vbf, self.Vbf_t))):
                    for h in range(2):
                        hp = slice(64 * h, 64 * h + 64)
                        col = (p * 3 + ti) * 64
                        s.op("pe", "transpose", bbk[hp, col:col + 64], X[hp, p, csl], self.cst[hp, 128 + 64 * h:128 + 64 * h + 64],
                             r=[X_t[p], self.cst_t], x=[bbx])
            s.op("act", "copy", self.TOKS[:, cc, :, :, :].rearrange("q a b c -> q (a b c)"), bbk[:, 0:384],
                 w=[self.TOKS_t[cc][0], self.TOKS_t[cc][1]], x=[bbx])
        k = 0
        for cc in range(NCK):
            csl = slice(cc * 64, cc * 64 + 64)
            for p in range(2):
                Mb, Mx = self.fb[2 + k % 2], self.fx[2 + k % 2]
                k += 1
                rt = [self.AT_t[p], self.BT_t[p], self.KT_t[p], self.RT_t[p]]
                for h in range(2):
                    hp = slice(64 * h, 64 * h + 64)
                    hc = slice(64 * h, 64 * h + 64)
                    a_ = self.AT[hp, p, csl]; b_ = self.BT[hp, p, csl]; k_ = self.KT[hp, p, csl]; r_ = self.RT[hp, p, csl]
                    s.mm(Mb[hp, 64 * h:64 * h + 64], b_, a_, True, True, r=rt, x=[Mx], skip_group_check=True)
                    s.mm(Mb[hp, 128 + 64 * h:128 + 64 * h + 64], a_, b_, True, True, r=rt, x=[Mx], skip_group_check=True)
                    s.mm(Mb[hp, 256:320], k_, a_, True, True, r=rt, x=[Mx], skip_group_check=True)
                    s.mm(Mb[hp, 320:384], b_, r_, True, True, r=rt, x=[Mx], skip_group_check=True)
                    s.mm(Mb[hp, 384:448], k_, r_, True, True, r=rt, x=[Mx], skip_group_check=True)
                s.op("dve", "tensor_tensor", self.MA[:, cc, p, :], Mb[:, 0:256], maskA, ALU.mult, r=[self.cst32_t],
                     w=[self.MA_t[cc][p]], x=[Mx])
                s.op("dve", "tensor_tensor", self.MB[:, cc, p, :], Mb[:, 256:448], maskB, ALU.mult, r=[self.cst32_t],
                     w=[self.MB_t[cc][p]], x=[Mx])
                pass
        def msk(li):
            return self.cst[:, (11 + 2 * li) * 128:(13 + 2 * li) * 128]
        ident2 = self.cst[:, 23 * 128:25 * 128]
        q = 0
        for cc in range(NCK):
            for p in range(2):
                og, og_t = self.OG[q % 2], self.OG_t[q % 2]
                q += 1
                s.op("pool", "tensor_tensor", og[:], self.MA[:, cc, p, :], msk(0), ALU.mult,
                     r=[self.MA_t[cc][p], self.cst_t], w=[og_t])
                s.op("pool", "tensor_tensor", self.Pm[:, cc, p, :], og[:], ident2, ALU.add, r=[og_t, self.cst_t],
                     w=[self.Pm_t[cc][p]])
        for li in range(1, 6):
            last = li == 5
            for cc in range(NCK):
                for p in range(2):
                    og, og_t = self.OG[q % 2], self.OG_t[q % 2]
                    ww, ww_t = self.WW[q % 2], self.WW_t[q % 2]
                    q += 1
                    Mb, Mx = self.fb[2 + k % 2], self.fx[2 + k % 2]
                    k += 1
                    pt = self.Pm_t[cc][p]
                    Tu = self.Pm[:, cc, p, 0:128]
                    Tl = self.Pm[:, cc, p, 128:256]
                    s.op("pool", "tensor_tensor", og[:], self.MA[:, cc, p, :], msk(li), ALU.mult,
                         r=[self.MA_t[cc][p], self.cst_t], w=[og_t])
                    s.mm(Mb[:, 0:128], og[:, 128:256], Tu, True, True, r=[og_t, pt], x=[Mx])
                    if not last:
                        s.mm(Mb[:, 128:256], og[:, 0:128], Tl, True, True, r=[og_t, pt], x=[Mx])
                    wn = 128 if last else 256
                    s.op("act", "copy", ww[:, 0:wn], Mb[:, 0:wn], w=[ww_t], x=[Mx])
                    s.mm(Mb[:, 256:384], ident, Tu, True, False, r=[self.cst_t, pt], x=[Mx])
                    s.mm(Mb[:, 256:384], Tl, ww[:, 0:128], False, True, r=[pt, ww_t], x=[Mx])
                    if not last:
                        s.mm(Mb[:, 384:512], ident, Tl, True, False, r=[self.cst_t, pt], x=[Mx])
                        s.mm(Mb[:, 384:512], Tu, ww[:, 128:256], False, True, r=[pt, ww_t], x=[Mx])
                    s.op("dve", "tensor_copy", self.Pm[:, cc, p, 0:wn], Mb[:, 256:256 + wn], w=[pt], x=[Mx])
        YT = [self.fb[2], self.fb[3]]
        YTx = [self.fx[2], self.fx[3]]
        SC = [self.fb[4], self.fb[5]]
        SCx = [self.fx[4], self.fx[5]]
        for cc in range(NCK):
            csl = slice(cc * 64, cc * 64 + 64)
            par = self.hpar
            self.hpar = 1 - par
            for p in range(2):
                sc, scx = SC[p], SCx[p]
                Hb = self.Hbf[:, p, par, :]; Hb_t = self.Hbf_t[p][par]
                Hn = self.Hbf[:, p, 1 - par, :]; Hn_t = self.Hbf_t[p][1 - par]
                tk = self.TOKS_t[cc][p]
                Btok = self.TOKS[:, cc, p, 0, :]; Ktok = self.TOKS[:, cc, p, 1, :]; Vtok = self.TOKS[:, cc, p, 2, :]
                mb_t = self.MB_t[cc][p]
                for h in range(2):
                    hp = slice(64 * h, 64 * h + 64)
                    s.mm(sc[hp, 0:64], self.AT[hp, p, csl], Hb[hp, :], True, False, r=[self.AT_t[p], Hb_t], x=[scx],
                         skip_group_check=True)
                    s.mm(sc[hp, 0:64], self.MB[hp, cc, p, 0:64], Vtok[hp, :], False, True, r=[mb_t, tk], x=[scx],
                         skip_group_check=True)
                s.op("act", "copy", self.Zs[:, p, :], sc[:, 0:64], w=[self.Zs_t[p]], x=[scx])
                s.mm(sc[:, 64:128], self.Pm[:, cc, p, 0:128], self.Zs[:, p, :], True, True,
                     r=[self.Pm_t[cc][p], self.Zs_t[p]], x=[scx])
                s.op("act", "copy", self.Us[:, p, :], sc[:, 64:128], w=[self.Us_t[p]], x=[scx])
                for h in range(2):
                    hp = slice(64 * h, 64 * h + 64)
                    s.mm(sc[hp, 128:192], Btok[hp, :], self.Us[hp, p, :], True, False, r=[tk, self.Us_t[p]], x=[scx],
                         skip_group_check=True)
                    s.mm(sc[hp, 128:192], Ktok[hp, :], Vtok[hp, :], False, True, r=[tk], x=[scx],
                         skip_group_check=True)
                gam = self.GI[:, p, cc * 64 + 63:cc * 64 + 64]
                s.op("pool", "tensor_scalar", self.HG[:, p, :], self.H[:, p, :], gam, 0.0, ALU.mult, ALU.add,
                     r=[self.H_t[p], self.GI_t[p]], w=[self.HG_t[p]])
                for h in range(2):
                    hp = slice(64 * h, 64 * h + 64)
                    s.mm(YT[p][hp, csl], Hb[hp, :], self.RT[hp, p, csl], True, False, r=[Hb_t, self.RT_t[p]],
                         x=[YTx[p]], skip_group_check=True)
                    s.mm(YT[p][hp, csl], self.Us[hp, p, :], self.MB[hp, cc, p, 64:128], False, False,
                         r=[self.Us_t[p], mb_t], x=[YTx[p]], skip_group_check=True)
                    s.mm(YT[p][hp, csl], Vtok[hp, :], self.MB[hp, cc, p, 128:192], False, True, r=[tk, mb_t],
                         x=[YTx[p]], skip_group_check=True)
                s.op("dve", "scalar_tensor_tensor", self.H[:, p, :], sc[:, 128:192], gam, self.HG[:, p, :], ALU.mult,
                     ALU.add, r=[self.HG_t[p], self.GI_t[p]], w=[self.H_t[p]], x=[scx])
                s.op("act", "copy", Hn, self.H[:, p, :], r=[self.H_t[p]], w=[Hn_t])
        blk64 = self.cb(2)
        for p in range(2):
            Ysb, Ysb_t = t32["SG"]; Yc, Yc_t = t32["A"]; sd, sd_t = t32["NU"]; yn, yn_t = t32["dv"]
            Ybf, Ybf_t = self.KK2, self.KK2_t
            Yc2, Yc2_t = self.rk, self.rk_t
            s.op("act", "copy", Ysb[:], YT[p][:, 0:HT], w=[Ysb_t], x=[YTx[p]])
            s.op("act", "copy", Ybf[:], Ysb[:], r=[Ysb_t], w=[Ybf_t])
            pb, px = self.nextbank()
            s.mm(pb[:, 0:HT], blk64, Ybf[:], True, True, r=[self.cst_t, Ybf_t], x=[px])
            s.op("dve", "tensor_tensor", Yc[:], Ysb[:], pb[:, 0:HT], ALU.subtract, r=[Ysb_t], w=[Yc_t], x=[px])
            s.op("act", "activation", Yc2[:], Yc[:], AF.Square, r=[Yc_t], w=[Yc2_t])
            pb, px = self.nextbank()
            s.mm(pb[:, 0:HT], blk64, Yc2[:], True, True, r=[self.cst_t, Yc2_t], x=[px])
            s.op("act", "activation", sd[:], pb[:, 0:HT], AF.Sqrt, bias=self.eps[:, 1:2], r=[self.eps_t], w=[sd_t],
                 x=[px])
            s.op("dve", "reciprocal", sd[:], sd[:], r=[sd_t], w=[sd_t])
            s.op("pool", "tensor_tensor", yn[:], Yc[:], sd[:], ALU.mult, r=[Yc_t, sd_t], w=[yn_t])
            s.op("dve", "tensor_scalar", yn[:], yn[:], self.pvc(PV_LNW + p), self.pvc(PV_LNB + p), ALU.mult, ALU.add,
                 r=[yn_t, self.pv_t], w=[yn_t])
            s.op("pool", "tensor_tensor", yn[:], yn[:], self.BON[:, p, :], ALU.add, r=[yn_t, self.BON_t[p]], w=[yn_t])
            s.op("dve", "tensor_tensor", self.outst[:, 2 + p, hsl], yn[:], self.GATE[:, p, :], ALU.mult,
                 r=[yn_t, self.GATE_t[p]], w=[self.outst_t[2 + p]])


TT = 512
D = 1024
DFF = 4096
RMS_EPS = 1e-5


def rms_tile(s, K, src, src_toks, nch, gains, out_bf, out_toks, sq, sq_tok, ps, ps_tok, rstd, rstd_tok, ones_bf,
             ones_tok, dim, ncols=TT):
    for i in range(nch):
        s.op("act", "activation", sq[i % len(sq)], src[i], AF.Square, r=[src_toks[i]], w=[sq_tok[i % len(sq)]])
        s.mm(ps, ones_bf, sq[i % len(sq)], i == 0, i == nch - 1, r=[ones_tok, sq_tok[i % len(sq)]], w=[ps_tok])
    s.op("act", "activation", rstd, ps, AF.Sqrt, bias=K["eps"], scale=1.0 / dim, r=[ps_tok, K["eps_tok"]],
         w=[rstd_tok])
    s.op("dve", "reciprocal", rstd, rstd, r=[rstd_tok], w=[rstd_tok])
    for i in range(nch):
        s.op("dve", "scalar_tensor_tensor", out_bf[i], src[i], gains[i], rstd, ALU.mult, ALU.mult,
             r=[src_toks[i], rstd_tok, K["g_tok"]], w=[out_toks[i]])


def build_phase_b(NT, final=False):
    nc = bass.Bass("TRN2", target_bir_lowering=False)
    ntile = NT // TT
    xT_d = nc.dram_tensor("xT", [D, NT], F32, kind="ExternalInput").ap()
    mixT_d = nc.dram_tensor("mixT", [D, NT], F32, kind="ExternalInput").ap()
    wout_d = nc.dram_tensor("w_out", [D, D], F32, kind="ExternalInput").ap()
    w1_d = nc.dram_tensor("w_ff_in", [D, DFF], F32, kind="ExternalInput").ap()
    w2_d = nc.dram_tensor("w_ff_out", [DFF, D], F32, kind="ExternalInput").ap()
    g_d = nc.dram_tensor("gains", [128, 20], F32, kind="ExternalInput").ap()
    out_d = nc.dram_tensor("xT_out", [D, NT], F32, kind="ExternalOutput").ap()

    with ExitStack() as es:
        s = Sched(nc, es)
        x = s.sb("x", [128, 8, NT], F32)
        x_tok = [[Tok("x%d_%d" % (c, j)) for j in range(ntile)] for c in range(8)]
        h2 = s.sb("h2", [128, 8, NT], BF16)
        h2_tok = [[Tok() for j in range(ntile)] for c in range(8)]
        NW = 3
        wbuf = [s.sb("w%d" % i, [128, 8, 1024], BF16) for i in range(NW)]
        wtok = [Tok("w%d" % i) for i in range(NW)]
        mix = s.sb("mix", [128, 8, TT], F32)
        mix_tok = [Tok() for c in range(8)]
        mixn = s.sb("mixn", [128, 8, TT], BF16)
        mixn_tok = [Tok() for c in range(8)]
        f = [s.sb("f%d" % i, [128, 8, TT], BF16) for i in range(2)]
        f_tok = [[Tok() for c in range(8)] for i in range(2)]
        sq = [s.sb("sq%d" % i, [128, TT], BF16) for i in range(2)]
        sq_tok = [Tok(), Tok()]
        sq32 = [s.sb("sq32_%d" % i, [128, TT], F32) for i in range(2)]
        sq32_tok = [Tok(), Tok()]
        rstd = s.sb("rstd", [128, TT], F32)
        rstd_tok = Tok()
        gains = s.sb("gains", [128, 20], F32)
        g_tok = Tok()
        ones = s.sb("ones", [128, 128], BF16)
        ones_tok = Tok()
        eps = s.sb("eps", [128, 1], F32)
        K = {"eps": eps[:, 0:1], "eps_tok": Tok(), "g_tok": g_tok}
        NP = 6
        pmm = [s.ps("pmm%d" % i, [128, TT]) for i in range(NP)]
        pmm_tok = [Tok() for i in range(NP)]
        pn = s.ps("pn", [128, TT])
        pn_tok = Tok()
        state = {"p": 0}

        def nextp():
            i = state["p"]
            state["p"] = (i + 1) % NP
            return pmm[i], pmm_tok[i]

        s.op("pool", "memset", ones[:], 1.0, w=[ones_tok])
        s.op("pool", "memset", eps[:], RMS_EPS, w=[K["eps_tok"]])
        s.dma("sp", gains[:], g_d, w=[g_tok])
        xv = xT_d.rearrange("(c p) n -> p c n", p=128)
        mv = mixT_d.rearrange("(c p) n -> p c n", p=128)
        ov = out_d.rearrange("(c p) n -> p c n", p=128)
        for j in range(ntile):
            for c in range(8):
                s.dma("sp", x[:, c, j * TT:(j + 1) * TT], xv[:, c, j * TT:(j + 1) * TT], w=[x_tok[c][j]])

        pieces = [wout_d.rearrange("(kc p) n -> p kc n", p=128)]
        for q in range(4):
            pieces.append(w1_d[:, q * 1024:(q + 1) * 1024].rearrange("(kc p) n -> p kc n", p=128))
            pieces.append(w2_d[q * 1024:(q + 1) * 1024, :].rearrange("(kc p) n -> p kc n", p=128))
        wstate = {"n": 0}

        def load_piece():
            i = wstate["n"]
            b = i % NW
            for kc in range(8):
                pass
            s.dma("pool", wbuf[b][:], pieces[i], w=[wtok[b]])
            wstate["n"] += 1
            return b

        loaded = [load_piece() for _ in range(3)]
        nloaded = 3

        bw = loaded[0]
        for j in range(ntile):
            sl = slice(j * TT, (j + 1) * TT)
            for c in range(8):
                s.dma("sp", mix[:, c, :], mv[:, c, sl], w=[mix_tok[c]])
            for grp in range(2):
                cs = [2 * grp, 2 * grp + 1]
                rms_tile(s, K, [mix[:, c, :] for c in cs], [mix_tok[c] for c in cs], 2,
                         [gains[:, c:c + 1] for c in cs], [mixn[:, c, :] for c in cs], [mixn_tok[c] for c in cs],
                         [q[:] for q in sq], sq_tok, pn[:], pn_tok, rstd[:], rstd_tok, ones[:], ones_tok, 256)
            for c in range(4, 8):
                s.op("act", "copy", mixn[:, c, :], mix[:, c, :], r=[mix_tok[c]], w=[mixn_tok[c]])
            for oc in range(8):
                p, pt = nextp()
                for kc in range(8):
                    s.mm(p[:], wbuf[bw][:, kc, oc * 128:(oc + 1) * 128], mixn[:, kc, :], kc == 0, kc == 7,
                         r=[wtok[bw], mixn_tok[kc]], w=[pt])
                s.op("dve", "tensor_tensor", x[:, oc, sl], x[:, oc, sl], p[:], ALU.add,
                     r=[pt, x_tok[oc][j]], w=[x_tok[oc][j]])
            rms_tile(s, K, [x[:, c, sl] for c in range(8)], [x_tok[c][j] for c in range(8)], 8,
                     [gains[:, 4 + c:5 + c] for c in range(8)], [h2[:, c, sl] for c in range(8)],
                     [h2_tok[c][j] for c in range(8)], [q[:] for q in sq], sq_tok, pn[:], pn_tok, rstd[:], rstd_tok,
                     ones[:], ones_tok, D)

        fi = 0
        for q in range(4):
            b1 = loaded[1 + 2 * q]
            b2 = loaded[2 + 2 * q]
            for j in range(ntile):
                sl = slice(j * TT, (j + 1) * TT)
                fb = f[fi % 2]
                ft = f_tok[fi % 2]
                fi += 1
                for fc in range(8):
                    p, pt = nextp()
                    for kc in range(8):
                        s.mm(p[:], wbuf[b1][:, kc, fc * 128:(fc + 1) * 128], h2[:, kc, sl], kc == 0, kc == 7,
                             r=[wtok[b1], h2_tok[kc][j]], w=[pt])
                    q32 = sq32[fc % 2]
                    q32t = sq32_tok[fc % 2]
                    s.op("act", "activation", q32[:], p[:], AF.Square, r=[pt], w=[q32t])
                    s.op("dve", "scalar_tensor_tensor", fb[:, fc, :], p[:], 0.0, q32[:], ALU.is_gt, ALU.mult,
                         r=[pt, q32t], w=[ft[fc]])
                for oc in range(8):
                    p, pt = nextp()
                    for fc in range(8):
                        s.mm(p[:], wbuf[b2][:, fc, oc * 128:(oc + 1) * 128], fb[:, fc, :], fc == 0, fc == 7,
                             r=[wtok[b2], ft[fc]], w=[pt])
                    s.op("dve", "tensor_tensor", x[:, oc, sl], x[:, oc, sl], p[:], ALU.add,
                         r=[pt, x_tok[oc][j]], w=[x_tok[oc][j]])
            while nloaded < len(pieces) and nloaded < 3 + 2 * (q + 1):
                loaded.append(load_piece())
                nloaded += 1

        fin = []
        for j in range(ntile):
            sl = slice(j * TT, (j + 1) * TT)
            if final:
                for i in range(8):
                    s.op("act", "activation", sq[i % 2][:], x[:, i, sl], AF.Square,
                         r=[x_tok[i][j]], w=[sq_tok[i % 2]])
                    s.mm(pn[:], ones[:], sq[i % 2][:], i == 0, i == 7, r=[ones_tok, sq_tok[i % 2]], w=[pn_tok])
                s.op("act", "activation", rstd[:], pn[:], AF.Sqrt, bias=K["eps"], scale=1.0 / D,
                     r=[pn_tok, K["eps_tok"]], w=[rstd_tok])
                s.op("dve", "reciprocal", rstd[:], rstd[:], r=[rstd_tok], w=[rstd_tok])
                for c in range(8):
                    s.op("dve", "scalar_tensor_tensor", mix[:, c, :], x[:, c, sl], gains[:, 12 + c:13 + c], rstd[:],
                         ALU.mult, ALU.mult,
                         r=[x_tok[c][j], rstd_tok, g_tok], w=[mix_tok[c]])
                    fin.append(s.dma("sp", ov[:, c, sl], mix[:, c, :], r=[mix_tok[c]]))
            else:
                for c in range(8):
                    fin.append(s.dma("sp", ov[:, c, sl], x[:, c, sl], r=[x_tok[c][j]]))
        s.finalize({"sp": fin})
    return nc

from concourse.bass_utils import run_bass_kernel_spmd

SEQ = 4096
BATCH = 4
DEPTH = 4


def make_consts():
    c = np.zeros((128, 25, 128), np.float32)
    i = np.arange(128)
    c[:, 0, :] = 1.0
    c[:, 1, :] = np.eye(128)
    blk = (i[:, None] // 64) == (i[None, :] // 64)
    c[:, 2, :] = blk / 64.0
    c[:, 3, :] = blk
    c[:, 4, :] = -(i[:, None] >= i[None, :]).astype(np.float32)
    c[:, 5, :] = (i[:, None] < i[None, :])
    bd_su = blk & (i[:, None] < i[None, :])
    bd_sl = blk & (i[:, None] > i[None, :])
    bd_ui = blk & (i[:, None] <= i[None, :])
    c[:, 6, :] = bd_su
    c[:, 7, :] = bd_sl
    c[:, 8, :] = bd_ui
    fold_su = bd_su[:, 0:64] | bd_su[:, 64:128]
    fold_ui = bd_ui[:, 0:64] | bd_ui[:, 64:128]
    c[:, 9, 0:64] = fold_su
    c[:, 9, 64:128] = fold_ui
    c[:, 10, 0:64] = fold_ui
    il = i % 64
    for li, g in enumerate((1, 2, 4, 8, 16, 32)):
        mg = blk & ((il[:, None] // (2 * g)) == (il[None, :] // (2 * g))) & ((il[:, None] // g) != (il[None, :] // g)) & (i[:, None] < i[None, :])
        c[:, 11 + 2 * li, :] = mg
        c[:, 12 + 2 * li, :] = mg.T
    c[:, 23, :] = np.eye(128)
    c[:, 24, :] = np.eye(128)
    return np.ascontiguousarray(c.reshape(128, 25 * 128))


def core_inputs_a(inp, l, g):
    w_in = inp["w_in"][l]
    o = 128 * g
    o2 = 256 * g
    rb = 1536
    cols = (list(range(o, o + 128)) + list(range(256 + o, 256 + o + 128)) + list(range(768 + o, 768 + o + 128))
            + list(range(1024 + o, 1024 + o + 128)) + list(range(rb + o2, rb + o2 + 256))
            + list(range(rb + 576 + o2, rb + 576 + o2 + 256)) + list(range(rb + 1088 + o2, rb + 1088 + o2 + 256))
            + list(range(rb + 512, rb + 576)) + list(range(rb + 1600, rb + 1664)) + list(range(rb + 1664, rb + 1824)))
    vd = inp["w_vmix_down"][l - 1] if l > 0 else np.zeros((1024, 32), np.float32)
    W = np.concatenate([w_in[:, cols], vd, w_in[:, 512 + o:512 + o + 128], w_in[:, 1280 + o:1280 + o + 128]], axis=1)
    pv = np.zeros((128, NPV), np.float32)
    pv[:, 0:8] = inp["norm_mix_g"][l].reshape(8, 128).T
    mu = inp["rw_mu"][l]
    mu_cols = np.concatenate([mu[o2:o2 + 256], mu[576 + o2:576 + o2 + 256], mu[1088 + o2:1088 + o2 + 256],
                              mu[512:576], mu[1600:1664], mu[1664:1824], np.zeros(96, np.float32)])
    pv[:, 8:17] = mu_cols.reshape(9, 128).T

    def two(v):
        return v[o2:o2 + 256].reshape(2, 128).T
    pv[:, 26:28] = two(inp["rw_w0"][l])
    pv[:, 28:30] = two(inp["rw_a0"][l])
    if l > 0:
        pv[:, 30:32] = two(inp["rw_v0"][l - 1])
    pv[:, 32:34] = two(inp["rw_k_k"][l])
    pv[:, 34:36] = two(inp["rw_k_a"][l])
    pv[:, 38:40] = two(inp["rw_r_k"][l])
    pv[:, 40:42] = two(inp["rw_ln_w"][l])
    pv[:, 42:44] = two(inp["rw_ln_b"][l])
    lw = np.zeros((128, 3, 256), np.float32)
    lw[0:64, 0] = inp["rw_w2"][l][:, o2:o2 + 256]
    lw[64:128, 0] = inp["rw_a2"][l][:, o2:o2 + 256]
    g2 = inp["rw_g2"][l][:, o2:o2 + 256]
    lw[:, 1] = g2[0:128]
    lw[0:32, 2] = g2[128:160]
    if l > 0:
        lw[32:64, 2] = inp["rw_v2"][l - 1][:, o2:o2 + 256]
    tab = inp["ca_rel_bias"][l][2 * g:2 * g + 2]
    kb = np.arange(640)[:, None]
    ql = np.arange(128)[None, :]
    idx = np.clip(kb - 512 - ql, -256, 63) + 256
    dch = kb // 64 - 8 - ql // 64
    vis = (dch >= -8) & (dch <= 0)
    cab = np.where(vis[None], tab[:, idx], np.float32(-30000.0)).astype(np.float32)
    cab = cab.reshape(2, 5, 128, 128).transpose(2, 0, 1, 3)
    return dict(w_core=np.ascontiguousarray(W), pvec=pv, lora_w=lw, ca_bias=np.ascontiguousarray(cab),
                consts=make_consts())


def kernel(**inputs):
    inp = {k: np.asarray(v, dtype=np.float32) for k, v in inputs.items()}
    S = SEQ
    xT = [np.ascontiguousarray(inp["x"][b].T) for b in range(BATCH)]
    vfirst = [None] * 8
    for l in range(DEPTH):
        nca = build_phase_a(S, l == 0)
        maps = []
        for core in range(8):
            b, g = core // 2, core % 2
            ci = core_inputs_a(inp, l, g)
            ci["xT"] = xT[b]
            if l > 0:
                ci["vfirst_in"] = vfirst[core]
            maps.append(ci)
        res = run_bass_kernel_spmd(nca, maps, core_ids=list(range(8))).results
        if l == 0:
            vfirst = [np.ascontiguousarray(res[c]["vfirst_out"]) for c in range(8)]
        mixT = []
        for b in range(BATCH):
            m0, m1 = res[2 * b]["mixhT"], res[2 * b + 1]["mixhT"]
            mixT.append(np.concatenate([m0[0:128], m1[0:128], m0[128:256], m1[128:256], m0[256:512], m1[256:512]],
                                       axis=0))
        ncb = build_phase_b(S // 2, final=(l == DEPTH - 1))
        gains = np.concatenate([inp["sb_out_g"][l].reshape(2, 128).T, inp["ca_out_g"][l].reshape(2, 128).T,
                                inp["norm_ffn_g"][l].reshape(8, 128).T, inp["norm_final_g"].reshape(8, 128).T],
                               axis=1)
        maps = []
        for core in range(8):
            b, hf = core // 2, core % 2
            sl = slice(hf * (S // 2), (hf + 1) * (S // 2))
            maps.append(dict(xT=np.ascontiguousarray(xT[b][:, sl]), mixT=np.ascontiguousarray(mixT[b][:, sl]),
                             w_out=inp["w_out"][l], w_ff_in=inp["w_ff_in"][l], w_ff_out=inp["w_ff_out"][l],
                             gains=np.ascontiguousarray(gains)))
        res = run_bass_kernel_spmd(ncb, maps, core_ids=list(range(8))).results
        xT = [np.concatenate([res[2 * b]["xT_out"], res[2 * b + 1]["xT_out"]], axis=1) for b in range(BATCH)]
    out = np.stack([x.T for x in xT], axis=0).astype(np.float32)
    return out
```
